# Optimizing a Trainium2 kernel written in Bass

```python
import math
import jax, jax.numpy as jnp
from jax import lax
import numpy as np

D_MODEL = 1024
BATCH = 2
SEQ = 8192
DEPTH = 2
DEC_BATCH = 32
DEC_SEQ = 1
PAST_LEN = 8192
PAGE_SIZE = 128

N_MIXERS = 2
N_RET_LAYERS = (DEPTH + 1) // 2
N_DIFF_LAYERS = DEPTH // 2
D_PLE = 256
D_FF = 4 * D_MODEL
EPS = 1e-6
NEG_INF = -1e30

RET_HEADS = 4
RET_DK = D_MODEL // RET_HEADS
RET_DV = 2 * RET_DK
RET_CHUNK = 128
ROPE_BASE = 10000.0

DIFF_HEADS = 8
DIFF_DH = D_MODEL // (2 * DIFF_HEADS)
DIFF_DV = 2 * DIFF_DH
Q_BLOCK = 128

kernel_name = 'retnet_diffattn_hybrid_step'


def _rmsnorm(x, g):
    xf = x.astype(jnp.float32)
    y = xf * lax.rsqrt(jnp.mean(xf * xf, axis=-1, keepdims=True) + EPS)
    return (y * g.astype(jnp.float32)).astype(x.dtype)


def _group_norm(o):
    mu = jnp.mean(o, axis=-1, keepdims=True)
    var = jnp.mean(jnp.square(o - mu), axis=-1, keepdims=True)
    return (o - mu) * lax.rsqrt(var + EPS)


def _rotary(x, start):
    T, dk = x.shape[1], x.shape[-1]
    angle = 1.0 / (ROPE_BASE ** jnp.linspace(0.0, 1.0, dk // 2, dtype=jnp.float32))
    angle = jnp.repeat(angle, 2)
    pos = start + jnp.arange(T, dtype=jnp.float32)
    th = pos[:, None] * angle[None, :]
    sin = jnp.sin(th)[None, :, None, :]
    cos = jnp.cos(th)[None, :, None, :]
    xf = x.astype(jnp.float32)
    rot = jnp.stack([-xf[..., 1::2], xf[..., 0::2]], axis=-1).reshape(xf.shape)
    return xf * cos + rot * sin


def _retention_scan(q, k, v, s0):
    B, T, H, dk = q.shape
    dv = v.shape[-1]
    C = RET_CHUNK if T % RET_CHUNK == 0 else T
    NC = T // C
    lg = jnp.log1p(-jnp.exp2(-5.0 - jnp.arange(H, dtype=jnp.float32)))
    idx = jnp.arange(C, dtype=jnp.float32)
    rel = idx[:, None] - idx[None, :]
    decay_in = jnp.where(rel[None] >= 0, jnp.exp(lg[:, None, None] * jnp.maximum(rel, 0.0)[None]), 0.0)
    dec_q = jnp.exp(lg[:, None] * (idx + 1.0)[None])
    dec_k = jnp.exp(lg[:, None] * (C - 1.0 - idx)[None])
    dec_c = jnp.exp(lg * C)

    def chunks(a):
        return a.astype(jnp.float32).reshape(B, NC, C, H, a.shape[-1]).transpose(1, 0, 3, 2, 4)

    def step(s, inp):
        qc, kc, vc = inp
        scores = jnp.einsum('bhid,bhjd->bhij', qc, kc) * decay_in[None]
        o = (jnp.einsum('bhij,bhje->bhie', scores, vc)
             + jnp.einsum('bhid,bhde->bhie', qc, s) * dec_q[None, :, :, None])
        s = s * dec_c[None, :, None, None] + jnp.einsum('bhjd,bhje->bhde', kc * dec_k[None, :, :, None], vc)
        return s, o

    s_fin, o = lax.scan(step, s0.astype(jnp.float32), (chunks(q), chunks(k), chunks(v)))
    o = o.transpose(1, 0, 3, 2, 4).reshape(B, T, H, dv)
    return o, s_fin


def _retention_mixer(h, s0, start, w_in, w_out):
    B, T, _ = h.shape
    proj = h @ w_in
    q = proj[..., :D_MODEL].reshape(B, T, RET_HEADS, RET_DK)
    k = proj[..., D_MODEL:2 * D_MODEL].reshape(B, T, RET_HEADS, RET_DK)
    v = proj[..., 2 * D_MODEL:2 * D_MODEL + RET_HEADS * RET_DV].reshape(B, T, RET_HEADS, RET_DV)
    g = proj[..., 2 * D_MODEL + RET_HEADS * RET_DV:]
    q = _rotary(q, start)
    k = _rotary(k, start) * (RET_DK ** -0.5)
    o, s_fin = _retention_scan(q, k, v, s0)
    o = _group_norm(o).reshape(B, T, RET_HEADS * RET_DV).astype(h.dtype)
    return (jax.nn.silu(g) * o) @ w_out, s_fin


def _diff_project(h, w_in):
    B, T, _ = h.shape
    proj = h @ w_in
    q = proj[..., :D_MODEL].reshape(B, T, DIFF_HEADS, 2, DIFF_DH)
    k = proj[..., D_MODEL:2 * D_MODEL].reshape(B, T, DIFF_HEADS, 2 * DIFF_DH)
    v = proj[..., 2 * D_MODEL:].reshape(B, T, DIFF_HEADS, DIFF_DV)
    return q, k, v


def _diff_attend_prompt(q, k, v, lam):
    B, T, H, _, dh = q.shape
    Qb = Q_BLOCK if T % Q_BLOCK == 0 else T
    NB = T // Qb
    scale = dh ** -0.5
    k1 = k[..., :dh].astype(jnp.float32)
    k2 = k[..., dh:].astype(jnp.float32)
    vf = v.astype(jnp.float32)
    kpos = jnp.arange(T, dtype=jnp.int32)

    def block(args):
        qb, start = args
        qb = qb.astype(jnp.float32)
        qpos = start + jnp.arange(Qb, dtype=jnp.int32)
        mask = (kpos[None, :] <= qpos[:, None])[None, None]
        s1 = jnp.einsum('bqhd,bkhd->bhqk', qb[..., 0, :], k1) * scale
        s2 = jnp.einsum('bqhd,bkhd->bhqk', qb[..., 1, :], k2) * scale
        a1 = jax.nn.softmax(jnp.where(mask, s1, NEG_INF), axis=-1)
        a2 = jax.nn.softmax(jnp.where(mask, s2, NEG_INF), axis=-1)
        return jnp.einsum('bhqk,bkhe->bqhe', a1 - lam * a2, vf)

    qblocks = q.reshape(B, NB, Qb, H, 2, dh).transpose(1, 0, 2, 3, 4, 5)
    starts = jnp.arange(NB, dtype=jnp.int32) * Qb
    o = lax.map(block, (qblocks, starts))
    return o.transpose(1, 0, 2, 3, 4).reshape(B, T, H, v.shape[-1])


def _diff_attend_sample(q, k_new, v_new, k_past, v_past, lam):
    T = q.shape[1]
    P = k_past.shape[1]
    dh = q.shape[-1]
    scale = dh ** -0.5
    qf = q.astype(jnp.float32)
    tpos = jnp.arange(T, dtype=jnp.int32)
    mask = (tpos[None, :] <= tpos[:, None])[None, None]

    def probs(i):
        sp = jnp.einsum('bqhd,bkhd->bhqk', qf[..., i, :], k_past[..., i * dh:(i + 1) * dh].astype(jnp.float32)) * scale
        sn = jnp.einsum('bqhd,bkhd->bhqk', qf[..., i, :], k_new[..., i * dh:(i + 1) * dh].astype(jnp.float32)) * scale
        sn = jnp.where(mask, sn, NEG_INF)
        return jax.nn.softmax(jnp.concatenate([sp, sn], axis=-1), axis=-1)

    a = probs(0) - lam * probs(1)
    return (jnp.einsum('bhqk,bkhe->bqhe', a[..., :P], v_past.astype(jnp.float32))
            + jnp.einsum('bhqk,bkhe->bqhe', a[..., P:], v_new.astype(jnp.float32)))


def _diff_out(o, subln, lam_init, w_out, dtype):
    B, T = o.shape[0], o.shape[1]
    o = _rmsnorm(o, subln) * (1.0 - lam_init)
    return o.reshape(B, T, DIFF_HEADS * DIFF_DV).astype(dtype) @ w_out


def _ffn(h, w_up, w_down):
    return jnp.square(jax.nn.relu(h @ w_up)) @ w_down


def _ple(y, p, g_norm, w_gate, w_proj):
    gate = jax.nn.sigmoid(_rmsnorm(y, g_norm) @ w_gate)
    return gate * (p @ w_proj)


def setup_inputs(seed: int = 0) -> dict:
    key = jax.random.key(seed)
    ks = jax.random.split(key, 26)
    f32 = jnp.float32
    n_pages = PAST_LEN // PAGE_SIZE
    n_used = DEC_BATCH * n_pages
    n_phys = n_used + n_used // 4

    def nrm(k, shape, scale=1.0):
        return jax.random.normal(k, shape, f32) * scale

    return {
        'x_prompt': nrm(ks[0], (BATCH, SEQ, D_MODEL)),
        'x_sample': nrm(ks[1], (DEC_BATCH, DEC_SEQ, D_MODEL)),
        'state_ret': nrm(ks[2], (N_RET_LAYERS, DEC_BATCH, RET_HEADS, RET_DK, RET_DV)),
        'cache_k': nrm(ks[3], (N_DIFF_LAYERS, n_phys, PAGE_SIZE, DIFF_HEADS, 2 * DIFF_DH)),
        'cache_v': nrm(ks[4], (N_DIFF_LAYERS, n_phys, PAGE_SIZE, DIFF_HEADS, DIFF_DV)),
        'page_table': jax.random.permutation(ks[5], n_phys)[:n_used].reshape(DEC_BATCH, n_pages).astype(jnp.int32),
        'p_prompt': nrm(ks[6], (DEPTH, BATCH, SEQ, D_PLE)),
        'p_sample': nrm(ks[7], (DEPTH, DEC_BATCH, DEC_SEQ, D_PLE)),
        'norm_mix': 1.0 + nrm(ks[8], (DEPTH, D_MODEL), 0.01),
        'ret_w_in': nrm(ks[9], (N_RET_LAYERS, D_MODEL, 2 * D_MODEL + 2 * RET_HEADS * RET_DV), D_MODEL ** -0.5),
        'ret_w_out': nrm(ks[10], (N_RET_LAYERS, RET_HEADS * RET_DV, D_MODEL), (RET_HEADS * RET_DV) ** -0.5),
        'diff_w_in': nrm(ks[11], (N_DIFF_LAYERS, D_MODEL, 3 * D_MODEL), D_MODEL ** -0.5),
        'diff_w_out': nrm(ks[12], (N_DIFF_LAYERS, DIFF_HEADS * DIFF_DV, D_MODEL), (DIFF_HEADS * DIFF_DV) ** -0.5),
        'diff_lambda_q1': nrm(ks[13], (N_DIFF_LAYERS, DIFF_DH), 0.1),
        'diff_lambda_k1': nrm(ks[14], (N_DIFF_LAYERS, DIFF_DH), 0.1),
        'diff_lambda_q2': nrm(ks[15], (N_DIFF_LAYERS, DIFF_DH), 0.1),
        'diff_lambda_k2': nrm(ks[16], (N_DIFF_LAYERS, DIFF_DH), 0.1),
        'diff_subln': 1.0 + nrm(ks[17], (N_DIFF_LAYERS, DIFF_DV), 0.01),
        'norm_ffn': 1.0 + nrm(ks[18], (DEPTH, D_MODEL), 0.01),
        'w_up': nrm(ks[19], (DEPTH, D_MODEL, D_FF), D_MODEL ** -0.5),
        'w_down': nrm(ks[20], (DEPTH, D_FF, D_MODEL), D_FF ** -0.5),
        'ple_norm': 1.0 + nrm(ks[21], (DEPTH, D_MODEL), 0.01),
        'w_ple_gate': nrm(ks[22], (DEPTH, D_MODEL, D_MODEL), D_MODEL ** -0.5),
        'w_ple_proj': nrm(ks[23], (DEPTH, D_PLE, D_MODEL), D_PLE ** -0.5),
        'final_norm': 1.0 + nrm(ks[24], (D_MODEL,), 0.01),
    }


def reference(x_prompt, x_sample, state_ret, cache_k, cache_v, page_table, p_prompt, p_sample,
              norm_mix, ret_w_in, ret_w_out, diff_w_in, diff_w_out,
              diff_lambda_q1, diff_lambda_k1, diff_lambda_q2, diff_lambda_k2, diff_subln,
              norm_ffn, w_up, w_down, ple_norm, w_ple_gate, w_ple_proj, final_norm):
    yp, ys = x_prompt, x_sample
    n_seq, n_pages = page_table.shape
    ret_p, ret_s, kp_rows, vp_rows, ks_rows, vs_rows = [], [], [], [], [], []
    for i in range(DEPTH):
        hp = _rmsnorm(yp, norm_mix[i])
        hs = _rmsnorm(ys, norm_mix[i])
        if i % N_MIXERS == 0:
            r = i // N_MIXERS
            s0 = jnp.zeros((yp.shape[0], RET_HEADS, RET_DK, RET_DV), jnp.float32)
            mp, sp = _retention_mixer(hp, s0, 0, ret_w_in[r], ret_w_out[r])
            ms, ss = _retention_mixer(hs, state_ret[r], PAST_LEN, ret_w_in[r], ret_w_out[r])
            ret_p.append(sp.astype(x_prompt.dtype))
            ret_s.append(ss.astype(state_ret.dtype))
        else:
            d = i // N_MIXERS
            lam_init = 0.8 - 0.6 * math.exp(-0.3 * i)
            lam = (jnp.exp(jnp.sum(diff_lambda_q1[d].astype(jnp.float32) * diff_lambda_k1[d].astype(jnp.float32)))
                   - jnp.exp(jnp.sum(diff_lambda_q2[d].astype(jnp.float32) * diff_lambda_k2[d].astype(jnp.float32)))
                   + lam_init)
            qp, kp, vp = _diff_project(hp, diff_w_in[d])
            op = _diff_attend_prompt(qp, kp, vp, lam)
            qs, kn, vn = _diff_project(hs, diff_w_in[d])
            k_past = cache_k[d][page_table].reshape(n_seq, n_pages * PAGE_SIZE, DIFF_HEADS, 2 * DIFF_DH)
            v_past = cache_v[d][page_table].reshape(n_seq, n_pages * PAGE_SIZE, DIFF_HEADS, DIFF_DV)
            os_ = _diff_attend_sample(qs, kn, vn, k_past, v_past, lam)
            mp = _diff_out(op, diff_subln[d], lam_init, diff_w_out[d], yp.dtype)
            ms = _diff_out(os_, diff_subln[d], lam_init, diff_w_out[d], ys.dtype)
            kp_rows.append(kp)
            vp_rows.append(vp)
            ks_rows.append(kn)
            vs_rows.append(vn)
        yp = yp + mp
        ys = ys + ms
        yp = yp + _ffn(_rmsnorm(yp, norm_ffn[i]), w_up[i], w_down[i])
        ys = ys + _ffn(_rmsnorm(ys, norm_ffn[i]), w_up[i], w_down[i])
        yp = yp + _ple(yp, p_prompt[i], ple_norm[i], w_ple_gate[i], w_ple_proj[i])
        ys = ys + _ple(ys, p_sample[i], ple_norm[i], w_ple_gate[i], w_ple_proj[i])
    y_prompt = _rmsnorm(yp, final_norm)
    y_sample = _rmsnorm(ys, final_norm)
    ret_state_prompt = jnp.stack(ret_p)
    ret_state_sample = jnp.stack(ret_s)
    k_rows_prompt = jnp.stack(kp_rows)
    v_rows_prompt = jnp.stack(vp_rows)
    k_rows_sample = jnp.stack(ks_rows)
    v_rows_sample = jnp.stack(vs_rows)
    return (y_prompt, y_sample, ret_state_prompt, ret_state_sample, k_rows_prompt, v_rows_prompt, k_rows_sample, v_rows_sample)
```

```python
import contextlib
import math
import types
import numpy as np
import ml_dtypes
import concourse.bass as bass
import concourse.mybir as mybir
from concourse.bass_utils import run_bass_kernel_spmd

F32 = mybir.dt.float32
BF16 = mybir.dt.bfloat16
I32 = mybir.dt.int32
AF = mybir.ActivationFunctionType
ALU = mybir.AluOpType
AX = mybir.AxisListType

D = 1024
EPS = 1e-6
NS = 32
NSL = 4
GAM = [1.0 - 2.0 ** (-5 - h) for h in range(4)]
LAM_INIT = 0.8 - 0.6 * math.exp(-0.3 * 1)
NEG = -200.0


def _freeze(fn):
    if fn is None or fn.__closure__ is None:
        return fn
    cells = []
    for c in fn.__closure__:
        try:
            cells.append(types.CellType(c.cell_contents))
        except ValueError:
            cells.append(c)
    return types.FunctionType(fn.__code__, fn.__globals__, fn.__name__, fn.__defaults__, tuple(cells))


class Buf:
    def __init__(self, name, multi=False):
        self.name = name
        self.multi = multi
        self.writers = []
        self.readers = []


class Op:
    __slots__ = ("eng", "fn", "reads", "writes", "dma", "deps", "signal", "evt", "inc")

    def __init__(self, eng, fn, reads, writes, dma, inc=None):
        self.eng = eng
        self.fn = fn
        self.reads = reads
        self.writes = writes
        self.dma = dma
        self.deps = []
        self.signal = False
        self.evt = None
        self.inc = inc


class Prog:
    ENGS = ("pe", "act", "dve", "pool", "sp")
    NDS = 18

    def __init__(self, nc):
        self.nc = nc
        self.ops = []
        self.last = {}
        self.dmas = []

    def add(self, eng, fn, reads=(), writes=(), dma=False, inc=None):
        op = Op(eng, _freeze(fn), tuple(reads), tuple(writes), dma, inc)
        deps = []
        for b in op.reads:
            deps.extend(b.writers)
        for b in op.writes:
            if b.multi:
                deps.extend(b.readers)
            else:
                deps.extend(b.writers)
                deps.extend(b.readers)
        seen = set()
        for d in deps:
            if d is op or id(d) in seen:
                continue
            seen.add(id(d))
            op.deps.append(d)
        for b in op.reads:
            b.readers.append(op)
        for b in op.writes:
            if b.multi:
                if b.readers:
                    b.readers = []
                    b.writers = []
                b.writers.append(op)
            else:
                b.writers = [op]
                b.readers = []
        self.ops.append(op)
        if dma:
            self.dmas.append(op)
        else:
            self.last[eng] = op
        return op

    def barrier(self):
        deps = list(self.last.values()) + list(self.dmas)
        self.dmas = []
        for e in self.ENGS:
            op = Op(e, None, (), (), False)
            op.deps = [d for d in deps]
            self.ops.append(op)

    def emit(self):
        nc = self.nc
        for op in self.ops:
            for d in op.deps:
                d.signal = True
            if op.dma:
                op.signal = True
        stack = contextlib.ExitStack()
        esem = {e: stack.enter_context(nc.semaphore("s_" + e)) for e in self.ENGS}
        NDS = self.NDS
        dsem = {e: [stack.enter_context(nc.semaphore("d_%s%d" % (e, i))) for i in range(NDS)]
                for e in ("sp", "act", "pool")}
        dcnt = {e: [0] * NDS for e in dsem}
        drr = {e: 0 for e in dsem}
        ecnt = {e: 0 for e in self.ENGS}
        for op in self.ops:
            if not op.signal or op.fn is None:
                continue
            if op.dma:
                k = drr[op.eng]
                drr[op.eng] = (k + 1) % NDS
                dcnt[op.eng][k] += op.inc or 16
                op.evt = (dsem[op.eng][k], dcnt[op.eng][k], ("d", op.eng, k))
            else:
                ecnt[op.eng] += 1
                op.evt = (esem[op.eng], ecnt[op.eng], ("e", op.eng))
        per = {e: [o for o in self.ops if o.eng == e] for e in self.ENGS}
        print('EMIT counts', ecnt, {e: max(v) for e, v in dcnt.items()}, {e: len(v) for e, v in per.items()}, flush=True)
        allsems = list(esem.values()) + [s for l in dsem.values() for s in l]

        def run(ename, eng):
            known = {}
            for op in per[ename]:
                need = {}
                for d in op.deps:
                    if d.evt is None:
                        continue
                    if (not d.dma) and d.eng == ename and not op.dma and ename == "pe":
                        continue
                    sem, val, key = d.evt
                    if known.get(key, 0) >= val:
                        continue
                    if key not in need or need[key][1] < val:
                        need[key] = (sem, val)
                for key, (sem, val) in need.items():
                    eng.wait_ge(sem, val)
                    known[key] = val
                if op.fn is None:
                    continue
                if op.dma:
                    sem, val, key = op.evt
                    prev = val - (op.inc or 16)
                    if prev > 0 and known.get(key, 0) < prev:
                        eng.wait_ge(sem, prev)
                        known[key] = prev
                ins = op.fn(eng)
                if op.signal:
                    ins.then_inc(op.evt[0], (op.inc or 16) if op.dma else 1)
            if ename in dsem:
                for k in range(NDS):
                    if dcnt[ename][k] > 0:
                        eng.wait_ge(dsem[ename][k], dcnt[ename][k])

        with nc.Block() as blk0:
            @blk0.gpsimd
            def _(g):
                for s in allsems:
                    g.sem_clear(s)
        with nc.Block() as block:
            @block.sync
            def _(e):
                run("sp", e)

            @block.scalar
            def _(e):
                run("act", e)

            @block.vector
            def _(e):
                run("dve", e)

            @block.gpsimd
            def _(e):
                run("pool", e)

            @block.tensor
            def _(e):
                run("pe", e)
        stack.close()


class TT:
    def __init__(self, t, name, multi=False):
        self.t = t
        self.b = Buf(name, multi)

    def __getitem__(self, k):
        return self.t[k]

    def ap(self):
        return self.t.ap()


class Ctx:
    pass


def build(SEQ, PAST, NPHYS):
    NTC = SEQ // 4
    NCH = NTC // 128
    NPG = PAST // 128
    GP = 128 // NPG
    NGRP = NS // GP
    nc = bass.Bass("TRN2", target_bir_lowering=False)
    P = Prog(nc)
    es = contextlib.ExitStack()
    uid = [0]
    import os
    KSTOP = int(os.environ.get('KSTOP', '99'))
    KSUB = os.environ.get('KSUB', '')

    class _Stop(Exception):
        pass

    def din(name, shape, dt=F32):
        return TT(nc.dram_tensor(name, list(shape), dt, kind="ExternalInput"), name)

    def dout(name, shape, dt=F32):
        return TT(nc.dram_tensor(name, list(shape), dt, kind="ExternalOutput"), name, multi=True)

    def dscr(name, shape, dt=F32):
        return TT(nc.dram_tensor(name, list(shape), dt), name, multi=True)

    def sb(name, shape, dt=F32, stack=None):
        uid[0] += 1
        nm = "%s_%d" % (name, uid[0])
        t = (stack or es).enter_context(nc.sbuf_tensor(nm, list(shape), dt))
        return TT(t, nm)

    xp = din("xp", [NTC, D]); xs = din("xs", [NSL, D])
    p0p = din("p0p", [NTC, 256]); p1p = din("p1p", [NTC, 256])
    p0s = din("p0s", [NSL, 256]); p1s = din("p1s", [NS, 256])
    st = din("st", [NSL, 1024, 512])
    ckc = din("ckc", [NPHYS, 16384]); cvc = din("cvc", [NPHYS, 16384])
    ptab = din("ptab", [128, NGRP], I32)
    w_rin = din("w_rin", [D, 6144]); w_rout = din("w_rout", [2048, D])
    w_din = din("w_din", [D, 3072]); w_dh = din("w_dh", [D, 384]); w_dout = din("w_dout", [D, D])
    w_up = [din("w_up%d" % i, [D, 4096]) for i in range(2)]
    w_dn = [din("w_dn%d" % i, [4096, D]) for i in range(2)]
    w_gt = [din("w_gt%d" % i, [D, D]) for i in range(2)]
    w_pj = [din("w_pj%d" % i, [256, D]) for i in range(2)]
    g_mix = din("g_mix", [2, D]); g_ffn = din("g_ffn", [2, D]); g_ple = din("g_ple", [2, D])
    g_fin = din("g_fin", [1, D]); g_sub = din("g_sub", [1, 128])
    lam4 = din("lam4", [4, 64])
    ident_d = din("ident", [128, 128], BF16)
    tabp = [din("tab%d" % i, [NTC, D]) for i in range(4)]
    tabs = din("tabs", [4, D])
    maskT_d = din("maskT", [128, 128], BF16)
    coef_d = din("coef", [1, 16]); attb_d = din("attb", [1, 12])
    maskeq_d = din("maskeq", [128, 4 * 128], BF16)
    selp_d = din("selp", [NGRP * NS, 128]); selT_d = din("selT", [NGRP * 128, NS])
    oh4_d = din("oh4", [NSL, NSL]); blk_d = din("blk", [128, 128])

    y_p = dout("y_p", [NTC, D]); y_s = dout("y_s", [NS, D])
    retp = dout("retp", [1024, 512]); rets = dout("rets", [NSL, 1024, 512])
    kp_o = dout("kp_o", [NTC, D]); vp_o = dout("vp_o", [NTC, D])
    ks_o = dout("ks_o", [NS, D]); vs_o = dout("vs_o", [NS, D])

    qkvg = dscr("qkvg", [NTC + NSL, 6144], BF16)
    gated = dscr("gated", [NTC + NSL, 2048], BF16)
    ymid = dscr("ymid", [NTC + NS, D])
    y1 = dscr("y1", [NTC, D])
    st_in = [dscr("st_in%d" % i, [512, 512]) for i in range(2)]; st_out = [dscr("st_out%d" % i, [4 * 512, 512]) for i in range(2)]
    q1 = dscr("q1", [NTC + NS, D], BF16)
    NKC = NTC // 256
    kv_in = [dscr("kv_in%d" % i, [256, 2048], BF16) for i in range(NKC)]
    kv_out = [dscr("kv_out%d" % i, [4 * 256, 2048], BF16) for i in range(NKC)]
    kvs = dscr("kvs", [NS, 2048])
    ys_in = dscr("ys_in", [NSL, D]); ys_mid = dscr("ys_mid", [4 * NSL, D]); ys_out = dscr("ys_out", [NS, D])
    os_in = dscr("os_in", [NS, 128]); os_mid = dscr("os_mid", [4 * NS, 128]); os_out = dscr("os_out", [8 * NS, 128])
    oall = dscr("oall", [NTC + NS, D], BF16)

    ident = sb("ident", [128, 128], BF16)
    P.add("sp", lambda e: e.dma_start(out=ident[:], in_=ident_d[:, :]), writes=[ident.b], dma=True)
    NWU = 10
    wst = [sb("wst", [128, 4, 512], F32) for _ in range(3)]
    wbf = [sb("wbf", [128, 4, 512], BF16) for _ in range(NWU)]
    wrr = [0, 0]
    psb = [TT(nc.alloc_psum_tensor("psb%d" % i, [128, 512], F32), "psb%d" % i) for i in range(8)]

    def ps16(i):
        return psb[i][:, :].bitcast(BF16)

    def dma(q, out, in_, reads, writes):
        P.add(q, lambda e: e.dma_start(out=out, in_=in_), reads=reads, writes=writes, dma=True)

    def wunit(W, r0, c0, ncols=512, nkt=4):
        s = wst[wrr[0] % 3]; wrr[0] += 1
        w = wbf[wrr[1] % NWU]; wrr[1] += 1
        src = W.t[r0:r0 + nkt * 128, c0:c0 + ncols].rearrange("(kt p) n -> p kt n", p=128)
        dma("sp", s[:, 0:nkt, 0:ncols], src, [W.b], [s.b])
        P.add("pool", lambda e: e.tensor_copy(out=w[:, 0:nkt, 0:ncols], in_=s[:, 0:nkt, 0:ncols]),
              reads=[s.b], writes=[w.b])
        return w

    def load_gain(name, src_ap, n=D, stack=None):
        g = sb(name, [128, n], F32, stack)
        dma("sp", g[:], src_ap.partition_broadcast(128), [], [g.b])
        return g

    cnt = [0]

    def evac_eng():
        cnt[0] += 1
        return "act" if cnt[0] % 2 else "dve"

    def copy(eng, out, in_, reads, writes):
        if eng == "act":
            P.add("act", lambda e: e.copy(out=out, in_=in_), reads=reads, writes=writes)
        else:
            P.add(eng, lambda e: e.tensor_copy(out=out, in_=in_), reads=reads, writes=writes)

    rn_junk = sb("rn_junk", [128, D], F32)
    rn_ss = [sb("rn_ss", [128, 1], F32) for _ in range(2)]
    rn_xn = [sb("rn_xn", [128, D], BF16) for _ in range(2)]
    rn_i = [0]
    qkvh = sb("qkvh", [NS, 384], F32)

    def rmsnorm_rstd(src, srcb, p, n, ss):
        P.add("act", lambda e: e.activation(out=rn_junk[0:p, 0:n], in_=src, func=AF.Square, accum_out=ss[0:p, :]),
              reads=[srcb], writes=[rn_junk.b, ss.b])
        P.add("dve", lambda e: e.tensor_scalar(out=ss[0:p, :], in0=ss[0:p, :], scalar1=1.0 / n, scalar2=EPS,
                                               op0=ALU.mult, op1=ALU.add), reads=[ss.b], writes=[ss.b])
        P.add("act", lambda e: e.activation(out=ss[0:p, :], in_=ss[0:p, :], func=AF.Sqrt), reads=[ss.b], writes=[ss.b])
        P.add("dve", lambda e: e.reciprocal(out=ss[0:p, :], in_=ss[0:p, :]), reads=[ss.b], writes=[ss.b])

    def transposeT(src, srcb, p, nkt, dst, dstb, col0, pbank):
        pv = ps16(pbank).rearrange("q (a b) -> q a b", a=8)
        for kt in range(nkt):
            P.add("pe", lambda e, kt=kt: e.transpose(out=pv[:, kt, 0:p], in_=src[:, kt * 128:(kt + 1) * 128],
                                                     identity=ident[0:p, 0:p]),
                  reads=[srcb, ident.b], writes=[psb[pbank].b])
        copy(evac_eng(), dst[:, 0:nkt, col0:col0 + p], pv[:, 0:nkt, 0:p], [psb[pbank].b], [dstb])

    def rmsnorm_T(src, srcb, p, gain, dstT, col0):
        k = rn_i[0] % 2; rn_i[0] += 1
        ss = rn_ss[k]; xn = rn_xn[k]
        rmsnorm_rstd(src, srcb, p, D, ss)
        P.add("dve", lambda e: e.scalar_tensor_tensor(out=xn[0:p, :], in0=src, scalar=ss[0:p, 0:1], in1=gain[0:p, :],
                                                      op0=ALU.mult, op1=ALU.mult),
              reads=[srcb, ss.b, gain.b], writes=[xn.b])
        transposeT(xn[0:p, :], xn.b, p, 8, dstT, dstT.b, col0, 6 + k)

    def tiles(nsamp):
        return [(t * 128, 128) for t in range(NCH)] + [(NTC, nsamp)]

    def ffn_ple(layer, nsamp, pp, ps_, final):
        with contextlib.ExitStack() as s2:
            gF = load_gain("gF", g_ffn.t[layer, :], stack=s2)
            gP = load_gain("gP", g_ple.t[layer, :], stack=s2)
            gE = load_gain("gE", g_fin.t[0, :], stack=s2) if final else None
            wpj = sb("wpj", [128, 2, D], BF16, s2)
            wpjs = sb("wpjs", [128, 2, D], F32, s2)
            dma("sp", wpjs[:], w_pj[layer].t[:, :].rearrange("(kt p) n -> p kt n", p=128), [], [wpjs.b])
            P.add("pool", lambda e: e.tensor_copy(out=wpj[:], in_=wpjs[:]), reads=[wpjs.b], writes=[wpj.b])
            NTG = 516 if nsamp <= 4 else 544
            yt = [sb("yt", [128, D], F32, s2) for _ in range(5)]
            xT = sb("xT", [128, 8, NTG], BF16, s2)
            hT = sb("hT", [128, 32, NTG], BF16, s2)
            rl = [sb("rl", [128, 512], BF16, s2) for _ in range(2)]
            pt_ = [sb("pt", [128, 256], F32, s2) for _ in range(2)]
            ptb = [sb("ptb", [128, 256], BF16, s2) for _ in range(2)]
            pT = sb("pT", [128, 2, NTG], BF16, s2)
            sg = [sb("sg", [128, 512], F32, s2) for _ in range(2)]
            tmp = [sb("tmp", [128, 512], F32, s2) for _ in range(2)]
            yo = [sb("yo", [128, D], F32, s2) for _ in range(2)]
            ngroups = NCH // 4 if NCH >= 4 else 1
            tpg = NCH // ngroups
            rr = 0
            for g in range(ngroups):
                tl = [(t * 128, 128) for t in range(g * tpg, (g + 1) * tpg)]
                if g == ngroups - 1:
                    tl.append((NTC, nsamp))
                cols = []
                c = 0
                for (r0, p) in tl:
                    cols.append(c); c += p
                ntok = c
                npr = tpg * 128
                for i, (r0, p) in enumerate(tl):
                    dma("sp", yt[i][0:p, :], ymid.t[r0:r0 + p, :], [ymid.b], [yt[i].b])
                    rmsnorm_T(yt[i][0:p, :], yt[i].b, p, gF, xT, cols[i])
                for n in range(8):
                    wu = [wunit(w_up[layer], kh * 512, n * 512) for kh in range(2)]
                    for fb in range(4):
                        segs = [(0, npr, 0)]
                        if ntok > npr:
                            segs.append((npr, ntok - npr, 1))
                        for (c0, w, which) in segs:
                            pb = psb[(rr % 2) if which == 0 else 5]
                            for kt in range(8):
                                P.add("pe", lambda e, kt=kt, pb=pb, c0=c0, w=w, fb=fb, wu=wu: e.matmul(
                                    pb[:, 0:w], lhsT=wu[kt // 4][:, kt % 4, fb * 128:(fb + 1) * 128],
                                    rhs=xT[:, kt, c0:c0 + w], start=(kt == 0), stop=(kt == 7)),
                                    reads=[wu[0].b, wu[1].b, xT.b], writes=[pb.b])
                            r = rl[rr % 2]
                            P.add("act", lambda e, r=r, pb=pb, w=w: e.activation(out=r[:, 0:w], in_=pb[:, 0:w], func=AF.Relu),
                                  reads=[pb.b], writes=[r.b])
                            P.add("dve", lambda e, r=r, w=w, c0=c0, n=n, fb=fb: e.tensor_tensor(
                                out=hT[:, n * 4 + fb, c0:c0 + w], in0=r[:, 0:w], in1=r[:, 0:w], op=ALU.mult),
                                reads=[r.b], writes=[hT.b])
                            rr += 1
                for n in range(2):
                    for kg in range(8):
                        wd = wunit(w_dn[layer], kg * 512, n * 512)
                        for i, (r0, p) in enumerate(tl):
                            for kt in range(4):
                                P.add("pe", lambda e, i=i, p=p, kt=kt, kg=kg, wd=wd: e.matmul(
                                    psb[i][0:p, :], lhsT=hT[:, kg * 4 + kt, cols[i]:cols[i] + p], rhs=wd[:, kt, :],
                                    start=(kg == 0 and kt == 0), stop=(kg == 7 and kt == 3)),
                                    reads=[hT.b, wd.b], writes=[psb[i].b])
                    for i, (r0, p) in enumerate(tl):
                        P.add("dve", lambda e, i=i, p=p, n=n: e.tensor_tensor(
                            out=yt[i][0:p, n * 512:(n + 1) * 512], in0=yt[i][0:p, n * 512:(n + 1) * 512],
                            in1=psb[i][0:p, :], op=ALU.add), reads=[yt[i].b, psb[i].b], writes=[yt[i].b])
                for i, (r0, p) in enumerate(tl):
                    rmsnorm_T(yt[i][0:p, :], yt[i].b, p, gP, xT, cols[i])
                    k = i % 2
                    psrc = (pp.t[r0:r0 + p, :] if r0 < NTC else ps_.t[0:p, :])
                    dma("sp", pt_[k][0:p, :], psrc, [], [pt_[k].b])
                    P.add("pool", lambda e, k=k, p=p: e.tensor_copy(out=ptb[k][0:p, :], in_=pt_[k][0:p, :]),
                          reads=[pt_[k].b], writes=[ptb[k].b])
                    transposeT(ptb[k][0:p, :], ptb[k].b, p, 2, pT, pT.b, cols[i], 6 + k)
                wg = [[wunit(w_gt[layer], kh * 512, n * 512) for kh in range(2)] for n in range(2)]
                for i, (r0, p) in enumerate(tl):
                    yo_ = yo[i % 2]
                    for n in range(2):
                        pg = psb[(2 * i + n) % 4]
                        pj = psb[4]
                        for kt in range(8):
                            P.add("pe", lambda e, i=i, p=p, kt=kt, n=n, pg=pg: e.matmul(
                                pg[0:p, :], lhsT=xT[:, kt, cols[i]:cols[i] + p], rhs=wg[n][kt // 4][:, kt % 4, :],
                                start=(kt == 0), stop=(kt == 7)),
                                reads=[xT.b, wg[n][0].b, wg[n][1].b], writes=[pg.b])
                        for kt in range(2):
                            P.add("pe", lambda e, i=i, p=p, kt=kt, n=n, pj=pj: e.matmul(
                                pj[0:p, :], lhsT=pT[:, kt, cols[i]:cols[i] + p], rhs=wpj[:, kt, n * 512:(n + 1) * 512],
                                start=(kt == 0), stop=(kt == 1)), reads=[pT.b, wpj.b], writes=[pj.b])
                        s_ = sg[n]; t_ = tmp[n]
                        P.add("act", lambda e, p=p, s_=s_, pg=pg: e.activation(out=s_[0:p, :], in_=pg[0:p, :], func=AF.Sigmoid),
                              reads=[pg.b], writes=[s_.b])
                        P.add("dve", lambda e, p=p, s_=s_, t_=t_, pj=pj: e.tensor_tensor(
                            out=t_[0:p, :], in0=s_[0:p, :], in1=pj[0:p, :], op=ALU.mult),
                            reads=[s_.b, pj.b], writes=[t_.b])
                        P.add("pool", lambda e, p=p, t_=t_, i=i, n=n, yo_=yo_: e.tensor_tensor(
                            out=yo_[0:p, n * 512:(n + 1) * 512], in0=yt[i][0:p, n * 512:(n + 1) * 512], in1=t_[0:p, :],
                            op=ALU.add), reads=[t_.b, yt[i].b], writes=[yo_.b])
                    if not final:
                        if r0 < NTC:
                            dma("pool", y1.t[r0:r0 + p, :], yo_[0:p, :], [yo_.b], [y1.b])
                        else:
                            dma("pool", ys_in.t[0:p, :], yo_[0:p, :], [yo_.b], [ys_in.b])
                    else:
                        k = rn_i[0] % 2; rn_i[0] += 1
                        ss = rn_ss[k]
                        rmsnorm_rstd(yo_[0:p, :], yo_.b, p, D, ss)
                        P.add("dve", lambda e, p=p, ss=ss, yo_=yo_: e.scalar_tensor_tensor(
                            out=yo_[0:p, :], in0=yo_[0:p, :], scalar=ss[0:p, 0:1], in1=gE[0:p, :], op0=ALU.mult, op1=ALU.mult),
                            reads=[yo_.b, ss.b, gE.b], writes=[yo_.b])
                        if r0 < NTC:
                            dma("pool", y_p.t[r0:r0 + p, :], yo_[0:p, :], [yo_.b], [y_p.b])
                        else:
                            dma("pool", y_s.t[0:p, :], yo_[0:p, :], [yo_.b], [y_s.b])
            P.barrier()

    def norm_all(src, srcs, nsamp, gain_ap, xnT, s2):
        gM = load_gain("gM", gain_ap, stack=s2)
        xt = [sb("xt", [128, D], F32, s2) for _ in range(2)]
        for i, (r0, p) in enumerate(tiles(nsamp)):
            x_ = xt[i % 2]
            sap = src.t[r0:r0 + p, :] if r0 < NTC else srcs.t[0:p, :]
            sbuf_ = src.b if r0 < NTC else srcs.b
            dma("sp", x_[0:p, :], sap, [sbuf_], [x_.b])
            rmsnorm_T(x_[0:p, :], x_.b, p, gM, xnT, r0)

    with contextlib.ExitStack() as s1:
        xnT = sb("xnT", [128, 8, NTC + NSL], BF16, s1)
        norm_all(xp, xs, NSL, g_mix.t[0, :], xnT, s1)
        ob = [sb("ob", [128, 512], BF16, s1) for _ in range(3)]
        oi = 0
        for n in range(12):
            wu = [wunit(w_rin, kh * 512, n * 512) for kh in range(2)]
            for i, (r0, p) in enumerate(tiles(NSL)):
                pb = psb[i % 4]
                for kt in range(8):
                    P.add("pe", lambda e, kt=kt, pb=pb, p=p, r0=r0, wu=wu: e.matmul(
                        pb[0:p, :], lhsT=xnT[:, kt, r0:r0 + p], rhs=wu[kt // 4][:, kt % 4, :],
                        start=(kt == 0), stop=(kt == 7)), reads=[xnT.b, wu[0].b, wu[1].b], writes=[pb.b])
                o_ = ob[oi % 3]; oi += 1
                copy(evac_eng(), o_[0:p, :], pb[0:p, :], [pb.b], [o_.b])
                dma("pool", qkvg.t[r0:r0 + p, n * 512:(n + 1) * 512], o_[0:p, :], [o_.b], [qkvg.b])
        P.barrier()

    if KSTOP == 1:
        P.emit(); es.close(); return nc
    with contextlib.ExitStack() as s1:
        maskT = sb("maskT", [128, 128], BF16, s1)
        dma("sp", maskT[:], maskT_d[:, :], [], [maskT.b])
        coef = load_gain("coef", coef_d.t[0, :], 16, s1)
        oh4 = sb("oh4", [NSL, NSL], F32, s1)
        dma("sp", oh4[:], oh4_d[:, :], [], [oh4.b])
        S = sb("S", [128, 8, 512], F32, s1)
        Sbf = sb("Sbf", [128, 8, 512], BF16, s1)
        Sg = [sb("Sg", [128, 512], F32, s1) for _ in range(2)]
        tb = [sb("tb", [128, D], F32, s1) for _ in range(4)]
        ch = sb("ch", [128, 6144], BF16, s1)
        t1 = sb("t1", [128, D], F32, s1); t2 = sb("t2", [128, D], F32, s1)
        qt = sb("qt", [128, D], BF16, s1); kt_ = sb("kt", [128, D], BF16, s1)
        qT = sb("qT", [128, 8, 128], BF16, s1); kT = sb("kT", [128, 8, 128], BF16, s1)
        sT = sb("sT", [128, 4, 128], BF16, s1)
        bst = sb("bst", [128, 4, 6], F32, s1); mv = sb("mv", [128, 4, 2], F32, s1)
        on = sb("on", [128, 2048], F32, s1); sgl = sb("sgl", [128, 2048], F32, s1)
        gtd = sb("gtd", [128, 2048], BF16, s1)

        def rotary(p, src_c0, tc, ts, dst):
            x = ch[0:p, src_c0:src_c0 + D]
            P.add("dve", lambda e: e.tensor_tensor(out=t1[0:p, :], in0=x, in1=tb[tc][0:p, :], op=ALU.mult),
                  reads=[ch.b, tb[tc].b], writes=[t1.b])
            xv = x.rearrange("q (m two) -> q m two", two=2)
            sv = tb[ts][0:p, :].rearrange("q (m two) -> q m two", two=2)
            tv = t2[0:p, :].rearrange("q (m two) -> q m two", two=2)
            P.add("pool", lambda e: e.tensor_tensor(out=tv[:, :, 0], in0=xv[:, :, 1], in1=sv[:, :, 0], op=ALU.mult),
                  reads=[ch.b, tb[ts].b], writes=[t2.b])
            P.add("pool", lambda e: e.tensor_tensor(out=tv[:, :, 1], in0=xv[:, :, 0], in1=sv[:, :, 1], op=ALU.mult),
                  reads=[ch.b, tb[ts].b, t2.b], writes=[t2.b])
            P.add("dve", lambda e: e.tensor_tensor(out=dst[0:p, :], in0=t1[0:p, :], in1=t2[0:p, :], op=ALU.add),
                  reads=[t1.b, t2.b], writes=[dst.b])

        def state_update(p, gch, first=False):
            for h in range(4):
                for dt in range(2):
                    j = h * 2 + dt
                    pb = psb[j % 4]
                    P.add("pe", lambda e, h=h, dt=dt, pb=pb: e.matmul(
                        pb[:, :], lhsT=kt_[0:p, h * 256 + dt * 128:h * 256 + (dt + 1) * 128],
                        rhs=ch[0:p, 2048 + h * 512:2048 + (h + 1) * 512], start=True, stop=True),
                        reads=[kt_.b, ch.b], writes=[pb.b])
                    sg_ = Sg[j % 2]
                    if first:
                        P.add("act", lambda e, j=j, pb=pb, h=h: e.mul(out=S[:, j, :], in_=pb[:, :], mul=gch[h]),
                              reads=[pb.b], writes=[S.b])
                    else:
                        P.add("pool", lambda e, j=j, sg_=sg_, h=h: e.tensor_scalar(
                            out=sg_[:], in0=S[:, j, :], scalar1=gch[h], scalar2=None, op0=ALU.mult),
                            reads=[S.b], writes=[sg_.b])
                        P.add("dve", lambda e, j=j, sg_=sg_, pb=pb, h=h: e.scalar_tensor_tensor(
                            out=S[:, j, :], in0=pb[:, :], scalar=gch[h], in1=sg_[:], op0=ALU.mult, op1=ALU.add),
                            reads=[pb.b, sg_.b], writes=[S.b])

        G128 = [g ** 128 for g in GAM]
        for t in range(NCH):
            r0 = t * 128
            dma("sp", ch[:, 1024:4096], qkvg.t[r0:r0 + 128, 1024:4096], [qkvg.b], [ch.b])
            for k in (2, 3):
                dma("sp", tb[k][:], tabp[k].t[r0:r0 + 128, :], [], [tb[k].b])
            rotary(128, 1024, 2, 3, kt_)
            state_update(128, G128, first=(t == 0))
        if KSUB == 'a':
            P.emit(); return nc
        for half in range(2):
            dma("pool", st_in[half].t[:, :].rearrange("(j p) n -> p j n", p=128), S[:, half * 4:(half + 1) * 4, :], [S.b], [st_in[half].b])
            P.add("pool", lambda e: e.collective_compute("AllGather", ALU.bypass, replica_groups=[[0, 1, 2, 3], [4, 5, 6, 7]],
                                                         ins=[st_in[half].ap()], outs=[st_out[half].ap()]),
                  reads=[st_in[half].b], writes=[st_out[half].b], dma=True, inc=1)
        Sr = on
        for r in range(4):
            for half in range(2):
                dma("sp", Sr[:, :].rearrange("q (j n) -> q j n", j=4),
                    st_out[half].t[r * 512:(r + 1) * 512, :].rearrange("(j p) n -> p j n", p=128),
                    [st_out[half].b], [Sr.b])
                for jj in range(4):
                    j = half * 4 + jj
                    h = j // 2
                    if r == 0:
                        P.add("dve", lambda e, j=j, jj=jj, h=h, r=r: e.tensor_scalar(
                            out=S[:, j, :], in0=Sr[:, jj * 512:(jj + 1) * 512], scalar1=coef[:, r * 4 + h:r * 4 + h + 1],
                            scalar2=None, op0=ALU.mult), reads=[Sr.b, coef.b], writes=[S.b])
                    else:
                        P.add("dve", lambda e, j=j, jj=jj, h=h, r=r: e.scalar_tensor_tensor(
                            out=S[:, j, :], in0=Sr[:, jj * 512:(jj + 1) * 512], scalar=coef[:, r * 4 + h:r * 4 + h + 1],
                            in1=S[:, j, :], op0=ALU.mult, op1=ALU.add), reads=[Sr.b, coef.b, S.b], writes=[S.b])
        P.add("pool", lambda e: e.tensor_copy(out=Sbf[:], in_=S[:]), reads=[S.b], writes=[Sbf.b])

        if KSUB == 'b':
            P.emit(); return nc
        def gn_gate(p, r0):
            P.add("act", lambda e: e.activation(out=sgl[0:p, :], in_=ch[0:p, 4096:6144], func=AF.Silu),
                  reads=[ch.b], writes=[sgl.b])
            for h in range(4):
                P.add("dve", lambda e, h=h: e.bn_stats(out=bst[0:p, h, :], in_=psb[h][0:p, :]),
                      reads=[psb[h].b], writes=[bst.b])
                P.add("dve", lambda e, h=h: e.bn_aggr(out=mv[0:p, h, :], in_=bst[0:p, h, :]),
                      reads=[bst.b], writes=[mv.b])
            P.add("dve", lambda e: e.tensor_scalar(out=mv[0:p, :, 1], in0=mv[0:p, :, 1], scalar1=EPS, scalar2=None,
                                                   op0=ALU.add), reads=[mv.b], writes=[mv.b])
            P.add("act", lambda e: e.activation(out=mv[0:p, :, 1], in_=mv[0:p, :, 1], func=AF.Sqrt),
                  reads=[mv.b], writes=[mv.b])
            P.add("dve", lambda e: e.reciprocal(out=mv[0:p, :, 1], in_=mv[0:p, :, 1]), reads=[mv.b], writes=[mv.b])
            for h in range(4):
                P.add("dve", lambda e, h=h: e.tensor_scalar(
                    out=on[0:p, h * 512:(h + 1) * 512], in0=psb[h][0:p, :], scalar1=mv[0:p, h, 0:1],
                    scalar2=mv[0:p, h, 1:2], op0=ALU.subtract, op1=ALU.mult),
                    reads=[psb[h].b, mv.b], writes=[on.b])
            P.add("pool", lambda e: e.tensor_tensor(out=gtd[0:p, :], in0=on[0:p, :], in1=sgl[0:p, :], op=ALU.mult),
                  reads=[on.b, sgl.b], writes=[gtd.b])
            dma("pool", gated.t[r0:r0 + p, :], gtd[0:p, :], [gtd.b], [gated.b])

        for t in range(NCH):
            r0 = t * 128
            dma("sp", ch[:, :], qkvg.t[r0:r0 + 128, :], [qkvg.b], [ch.b])
            for k in range(4):
                dma("sp", tb[k][:], tabp[k].t[r0:r0 + 128, :], [], [tb[k].b])
            rotary(128, 0, 0, 1, qt)
            rotary(128, 1024, 2, 3, kt_)
            transposeT(qt[:, :], qt.b, 128, 8, qT, qT.b, 0, 6)
            transposeT(kt_[:, :], kt_.b, 128, 8, kT, kT.b, 0, 7)
            for h in range(4):
                for dt in range(2):
                    P.add("pe", lambda e, h=h, dt=dt: e.matmul(
                        psb[4][:, h * 128:(h + 1) * 128], lhsT=kT[:, h * 2 + dt, :], rhs=qT[:, h * 2 + dt, :],
                        start=(dt == 0), stop=(dt == 1)), reads=[kT.b, qT.b], writes=[psb[4].b])
            P.add("dve", lambda e: e.tensor_tensor(
                out=sT[:], in0=psb[4][:, :].rearrange("q (h i) -> q h i", h=4),
                in1=maskT[:, :].unsqueeze(1).broadcast_to([128, 4, 128]), op=ALU.mult),
                reads=[psb[4].b, maskT.b], writes=[sT.b])
            for h in range(4):
                P.add("pe", lambda e, h=h: e.matmul(psb[h][:, :], lhsT=sT[:, h, :], rhs=ch[:, 2048 + h * 512:2048 + (h + 1) * 512],
                                                    start=True, stop=False), reads=[sT.b, ch.b], writes=[psb[h].b])
                for dt in range(2):
                    P.add("pe", lambda e, h=h, dt=dt: e.matmul(psb[h][:, :], lhsT=qT[:, h * 2 + dt, :], rhs=Sbf[:, h * 2 + dt, :],
                                                               start=False, stop=(dt == 1)),
                          reads=[qT.b, Sbf.b], writes=[psb[h].b])
            gn_gate(128, r0)
            state_update(128, G128)
            P.add("pool", lambda e: e.tensor_copy(out=Sbf[:], in_=S[:]), reads=[S.b], writes=[Sbf.b])
        if KSUB == 'c':
            P.emit(); return nc
        dma("pool", retp.t[:, :].rearrange("(j p) n -> p j n", p=128), S[:], [S.b], [retp.b])

        p = NSL
        dma("sp", ch[0:p, :], qkvg.t[NTC:NTC + p, :], [qkvg.b], [ch.b])
        for k in range(4):
            dma("sp", tb[k][0:p, :], tabs.t[k, :].partition_broadcast(p), [], [tb[k].b])
        rotary(p, 0, 0, 1, qt)
        rotary(p, 1024, 2, 3, kt_)
        transposeT(qt[0:p, :], qt.b, p, 8, qT, qT.b, 0, 6)
        vm = sb("vm", [NSL, 2048], BF16, s1)
        qm = sb("qm", [128, 8, NSL], BF16, s1)
        for s in range(NSL):
            dma("sp", S[:], st.t[s, :, :].rearrange("(j p) n -> p j n", p=128), [], [S.b])
            P.add("dve", lambda e, s=s: e.tensor_scalar(out=vm[:, :], in0=ch[0:p, 2048:4096], scalar1=oh4[:, s:s + 1],
                                                        scalar2=None, op0=ALU.mult), reads=[ch.b, oh4.b], writes=[vm.b])
            for j in range(8):
                h = j // 2
                pb = psb[4 + j % 2]
                P.add("pe", lambda e, j=j, h=h, pb=pb: e.matmul(
                    pb[:, :], lhsT=kt_[0:p, j * 128:(j + 1) * 128], rhs=vm[0:p, h * 512:(h + 1) * 512],
                    start=True, stop=True), reads=[kt_.b, vm.b], writes=[pb.b])
                P.add("dve", lambda e, j=j, pb=pb: e.tensor_tensor(out=S[:, j, :], in0=S[:, j, :], in1=pb[:, :], op=ALU.add),
                      reads=[S.b, pb.b], writes=[S.b])
            P.add("pool", lambda e: e.tensor_copy(out=Sbf[:], in_=S[:]), reads=[S.b], writes=[Sbf.b])
            P.add("pool", lambda e: e.memset(qm[:], 0.0), reads=[], writes=[qm.b])
            P.add("pool", lambda e, s=s: e.tensor_copy(out=qm[:, :, s:s + 1], in_=qT[:, :, s:s + 1]),
                  reads=[qT.b, qm.b], writes=[qm.b])
            for h in range(4):
                for dt in range(2):
                    P.add("pe", lambda e, h=h, dt=dt, s=s: e.matmul(
                        psb[h][0:p, :], lhsT=qm[:, h * 2 + dt, :], rhs=Sbf[:, h * 2 + dt, :],
                        start=(s == 0 and dt == 0), stop=(s == NSL - 1 and dt == 1)),
                        reads=[qm.b, Sbf.b], writes=[psb[h].b])
            for j in range(8):
                h = j // 2
                P.add("act", lambda e, j=j, h=h: e.mul(out=S[:, j, :], in_=S[:, j, :], mul=GAM[h]),
                      reads=[S.b, Sbf.b], writes=[S.b])
            dma("pool", rets.t[s, :, :].rearrange("(j p) n -> p j n", p=128), S[:], [S.b], [rets.b])
        gn_gate(p, NTC)
        P.barrier()

    if KSTOP == 2:
        P.emit(); es.close(); return nc
    with contextlib.ExitStack() as s1:
        wo = [[wunit(w_rout, kq * 512, n * 512) for kq in range(4)] for n in range(2)]
        gt_ = [sb("gt", [128, 2048], BF16, s1) for _ in range(2)]
        gT = [sb("gT", [128, 16, 128], BF16, s1) for _ in range(2)]
        xr = [sb("xr", [128, D], F32, s1) for _ in range(2)]
        for i, (r0, p) in enumerate(tiles(NSL)):
            k = i % 2
            dma("sp", gt_[k][0:p, :], gated.t[r0:r0 + p, :], [gated.b], [gt_[k].b])
            dma("sp", xr[k][0:p, :], (xp.t[r0:r0 + p, :] if r0 < NTC else xs.t[0:p, :]), [], [xr[k].b])
            for half in range(2):
                pv = ps16(6 + half).rearrange("q (a b) -> q a b", a=8)
                for kt in range(8):
                    P.add("pe", lambda e, kt=kt, half=half, k=k, p=p, pv=pv: e.transpose(
                        out=pv[:, kt, 0:p], in_=gt_[k][0:p, (half * 8 + kt) * 128:(half * 8 + kt + 1) * 128],
                        identity=ident[0:p, 0:p]), reads=[gt_[k].b, ident.b], writes=[psb[6 + half].b])
                copy(evac_eng(), gT[k][:, half * 8:(half + 1) * 8, 0:p], pv[:, :, 0:p], [psb[6 + half].b], [gT[k].b])
            for n in range(2):
                pb = psb[(2 * i + n) % 4]
                for kt in range(16):
                    P.add("pe", lambda e, kt=kt, n=n, k=k, p=p, pb=pb: e.matmul(
                        pb[0:p, :], lhsT=gT[k][:, kt, 0:p], rhs=wo[n][kt // 4][:, kt % 4, :],
                        start=(kt == 0), stop=(kt == 15)), reads=[gT[k].b] + [w.b for w in wo[n]], writes=[pb.b])
                P.add("dve", lambda e, n=n, k=k, p=p, pb=pb: e.tensor_tensor(
                    out=xr[k][0:p, n * 512:(n + 1) * 512], in0=xr[k][0:p, n * 512:(n + 1) * 512], in1=pb[0:p, :], op=ALU.add),
                    reads=[xr[k].b, pb.b], writes=[xr[k].b])
            dma("pool", ymid.t[r0:r0 + p, :], xr[k][0:p, :], [xr[k].b], [ymid.b])
        P.barrier()

    if KSTOP == 3:
        P.emit(); es.close(); return nc
    ffn_ple(0, NSL, p0p, p0s, final=False)

    if KSTOP == 4:
        P.emit(); es.close(); return nc
    def gather8(a, m, o):
        P.add("pool", lambda e: e.collective_compute("AllGather", ALU.bypass, replica_groups=[[0, 1, 2, 3], [4, 5, 6, 7]],
                                                     ins=[a.ap()], outs=[m.ap()]), reads=[a.b], writes=[m.b], dma=True, inc=1)
        P.add("pool", lambda e: e.collective_compute("AllGather", ALU.bypass, replica_groups=[[0, 4], [1, 5], [2, 6], [3, 7]],
                                                     ins=[m.ap()], outs=[o.ap()]), reads=[m.b], writes=[o.b], dma=True, inc=1)

    gather8(ys_in, ys_mid, ys_out)

    with contextlib.ExitStack() as s1:
        xnT = sb("xnT1", [128, 8, NTC + NS], BF16, s1)
        norm_all(y1, ys_out, NS, g_mix.t[1, :], xnT, s1)
        if KSUB == 'e':
            P.emit(); return nc
        of = [sb("of", [128, 512], F32, s1) for _ in range(3)]
        ob = [sb("ob1", [128, 512], BF16, s1) for _ in range(3)]
        oi = 0
        for n in range(6):
            wu = [wunit(w_din, kh * 512, n * 512) for kh in range(2)]
            for i, (r0, p) in enumerate(tiles(NS)):
                pb = psb[i % 4]
                for kt in range(8):
                    P.add("pe", lambda e, kt=kt, pb=pb, p=p, r0=r0, wu=wu: e.matmul(
                        pb[0:p, :], lhsT=xnT[:, kt, r0:r0 + p], rhs=wu[kt // 4][:, kt % 4, :],
                        start=(kt == 0), stop=(kt == 7)), reads=[xnT.b, wu[0].b, wu[1].b], writes=[pb.b])
                o_ = ob[oi % 3]; f_ = of[oi % 3]; oi += 1
                samp = r0 >= NTC
                if n < 2:
                    copy("dve", o_[0:p, :], pb[0:p, :], [pb.b], [o_.b])
                    dma("pool", q1.t[r0:r0 + p, n * 512:(n + 1) * 512], o_[0:p, :], [o_.b], [q1.b])
                else:
                    c0 = (n - 2) * 512
                    KD = int(os.environ.get('KDBG', '0'))
                    dst = (ks_o if n < 4 else vs_o) if samp else (kp_o if n < 4 else vp_o)
                    rr0 = 0 if samp else r0
                    if KD != 1:
                        copy("act", f_[0:p, :], pb[0:p, :], [pb.b], [f_.b])
                        dma("pool", dst.t[rr0:rr0 + p, (n % 2) * 512:(n % 2 + 1) * 512], f_[0:p, :], [f_.b], [dst.b])
                    if samp:
                        pass
                    elif KD != 2:
                        copy("dve", o_[0:p, :], f_[0:p, :], [f_.b], [o_.b])
                        dma("pool", kv_in[r0 // 256].t[r0 % 256:r0 % 256 + p, c0:c0 + 512], o_[0:p, :], [o_.b], [kv_in[r0 // 256].b])
        if KSUB == 'f':
            P.emit(); return nc
        wh = [wunit(w_dh, kh * 512, 0, ncols=384) for kh in range(2)]
        for kt in range(8):
            P.add("pe", lambda e, kt=kt: e.matmul(psb[5][0:NS, 0:384], lhsT=xnT[:, kt, NTC:NTC + NS],
                                                  rhs=wh[kt // 4][:, kt % 4, 0:384], start=(kt == 0), stop=(kt == 7)),
                  reads=[xnT.b, wh[0].b, wh[1].b], writes=[psb[5].b])
        copy("act", qkvh[:, :], psb[5][0:NS, 0:384], [psb[5].b], [qkvh.b])
        if KSUB == 'g':
            P.emit(); return nc
        for ci in range(NKC):
            P.add("pool", lambda e: e.collective_compute("AllGather", ALU.bypass, replica_groups=[[0, 1, 2, 3], [4, 5, 6, 7]],
                                                         ins=[kv_in[ci].ap()], outs=[kv_out[ci].ap()]),
                  reads=[kv_in[ci].b], writes=[kv_out[ci].b], dma=True, inc=1)
        P.barrier()

    if KSTOP == 5:
        P.emit(); es.close(); return nc
    lamt2 = sb("lamt", [128, 256], F32)
    dma("sp", lamt2[:], lam4.t[:, :].rearrange("a b -> (a b)").partition_broadcast(128), [], [lamt2.b])

    class _L:
        b = lamt2.b

        def __getitem__(self, k):
            return lamt2[:, :].rearrange("q (a b) -> q a b", a=4)[k]
    lamt = _L()
    lpr = sb("lpr", [128, 2, 64], F32); lsum = sb("lsum", [128, 2], F32)
    lam = sb("lam", [128, 1], F32); nlam = sb("nlam", [128, 1], F32)
    P.add("dve", lambda e: e.tensor_tensor(out=lpr[:, 0, :], in0=lamt[:, 0, :], in1=lamt[:, 1, :], op=ALU.mult),
          reads=[lamt.b], writes=[lpr.b])
    P.add("dve", lambda e: e.tensor_tensor(out=lpr[:, 1, :], in0=lamt[:, 2, :], in1=lamt[:, 3, :], op=ALU.mult),
          reads=[lamt.b, lpr.b], writes=[lpr.b])
    P.add("dve", lambda e: e.tensor_reduce(out=lsum[:], in_=lpr[:], axis=AX.X, op=ALU.add), reads=[lpr.b], writes=[lsum.b])
    P.add("act", lambda e: e.activation(out=lsum[:], in_=lsum[:], func=AF.Exp), reads=[lsum.b], writes=[lsum.b])
    P.add("dve", lambda e: e.tensor_tensor(out=lam[:], in0=lsum[:, 0:1], in1=lsum[:, 1:2], op=ALU.subtract),
          reads=[lsum.b], writes=[lam.b])
    P.add("dve", lambda e: e.tensor_scalar(out=lam[:], in0=lam[:], scalar1=LAM_INIT, scalar2=None, op0=ALU.add),
          reads=[lam.b], writes=[lam.b])
    P.add("dve", lambda e: e.tensor_scalar(out=nlam[:], in0=lam[:], scalar1=-1.0, scalar2=None, op0=ALU.mult),
          reads=[lam.b], writes=[nlam.b])
    gsub = load_gain("gsub", g_sub.t[0, :], 128)
    P.add("dve", lambda e: e.tensor_scalar(out=gsub[:], in0=gsub[:], scalar1=1.0 - LAM_INIT, scalar2=None, op0=ALU.mult),
          reads=[gsub.b], writes=[gsub.b])

    def subln_store(p, src, srcb, dst_ap, dstb, osb, q):
        k = rn_i[0] % 2; rn_i[0] += 1
        ss = rn_ss[k]
        rmsnorm_rstd(src, srcb, p, 128, ss)
        P.add("dve", lambda e: e.scalar_tensor_tensor(out=osb[0:p, :], in0=src, scalar=ss[0:p, 0:1], in1=gsub[0:p, :],
                                                      op0=ALU.mult, op1=ALU.mult), reads=[srcb, ss.b, gsub.b], writes=[osb.b])
        dma(q, dst_ap, osb[0:p, :], [osb.b], [dstb])

    if KSTOP == 6:
        P.emit(); es.close(); return nc
    NKB = 4 * NCH
    with contextlib.ExitStack() as s1:
        attb = load_gain("attb", attb_d.t[0, :], 12, s1)
        maskeq = sb("maskeq", [128, 4, 128], BF16, s1)
        dma("sp", maskeq[:], maskeq_d.t[:, :].rearrange("q (r i) -> q r i", r=4), [], [maskeq.b])
        qh = sb("qh", [128, NCH, 128], BF16, s1)
        kh_ = sb("kh", [128, NKB, 128], BF16, s1)
        QT = sb("QT", [128, NCH, 128], BF16, s1)
        QTm = [sb("QTm", [128, NCH, 128], BF16, s1) for _ in range(2)]
        KT = sb("KT", [128, NKB, 128], BF16, s1)
        Va = sb("Va", [128, NKB, 130], BF16, s1)
        P.add("pool", lambda e: e.memset(Va[:, :, 128:130], 1.0), reads=[], writes=[Va.b])
        Va.b.multi = True
        kh_.b.multi = True
        E = [sb("E", [128, 2, 128], BF16, s1) for _ in range(3)]
        rc = sb("rc", [128, 2], F32, s1)
        o1 = sb("o1", [128, 128], F32, s1); o2 = [sb("o2", [128, 128], F32, s1) for _ in range(2)]
        osb = [sb("osb", [128, 128], BF16, s1) for _ in range(2)]
        ei = 0
        for h in range(8):
            dma("sp", qh[:], q1.t[0:NTC, h * 128:(h + 1) * 128].rearrange("(t p) d -> p t d", p=128), [q1.b], [qh.b])
            for ci in range(NKC):
                for r in range(4):
                    kb0 = r * NCH + ci * 2
                    dma("sp", kh_[:, kb0:kb0 + 2, :],
                        kv_out[ci].t[r * 256:(r + 1) * 256, h * 128:(h + 1) * 128].rearrange("(t p) d -> p t d", p=128),
                        [kv_out[ci].b], [kh_.b])
                    dma("sp", Va[:, kb0:kb0 + 2, 0:128],
                        kv_out[ci].t[r * 256:(r + 1) * 256, 1024 + h * 128:1024 + (h + 1) * 128].rearrange("(t p) d -> p t d", p=128),
                        [kv_out[ci].b], [Va.b])
            for t0 in range(0, NCH, 8):
                n = min(8, NCH - t0)
                pv = ps16(6).rearrange("q (a b) -> q a b", a=8)
                for a in range(n):
                    P.add("pe", lambda e, a=a, t0=t0, pv=pv: e.transpose(out=pv[:, a, :], in_=qh[:, t0 + a, :], identity=ident[:]),
                          reads=[qh.b, ident.b], writes=[psb[6].b])
                copy(evac_eng(), QT[:, t0:t0 + n, :], pv[:, 0:n, :], [psb[6].b], [QT.b])
            for t0 in range(0, NKB, 8):
                n = min(8, NKB - t0)
                pv = ps16(7).rearrange("q (a b) -> q a b", a=8)
                for a in range(n):
                    P.add("pe", lambda e, a=a, t0=t0, pv=pv: e.transpose(out=pv[:, a, :], in_=kh_[:, t0 + a, :], identity=ident[:]),
                          reads=[kh_.b, ident.b], writes=[psb[7].b])
                copy(evac_eng(), KT[:, t0:t0 + n, :], pv[:, 0:n, :], [psb[7].b], [KT.b])
            for mm in range(2):
                P.add("pool", lambda e, mm=mm: e.tensor_copy(out=QTm[mm][:], in_=QT[:]), reads=[QT.b], writes=[QTm[mm].b])
                lo = 64 if mm == 0 else 0
                P.add("pool", lambda e, mm=mm, lo=lo: e.memset(QTm[mm][lo:lo + 64, :, :], 0.0), reads=[QTm[mm].b], writes=[QTm[mm].b])
            if KSUB == 'h':
                P.emit(); return nc
            for i in range(NCH):
                accs = [psb[4], psb[5]]
                for kb in range(NKB):
                    r = kb // NCH; j = kb % NCH
                    typ = 0 if j < i else (1 if j == i else 2)
                    pS = psb[kb % 4]
                    for m in range(2):
                        P.add("pe", lambda e, m=m, kb=kb, i=i, pS=pS: e.matmul(
                            pS[:, m * 128:(m + 1) * 128], lhsT=KT[:, kb, :], rhs=QTm[m][:, i, :],
                            start=True, stop=True), reads=[KT.b, QTm[m].b], writes=[pS.b])
                    E_ = E[ei % 3]; ei += 1
                    P.add("act", lambda e, E_=E_, pS=pS, typ=typ, r=r: e.activation(
                        out=E_[:].rearrange("q a b -> q (a b)"), in_=pS[:, 0:256], func=AF.Exp,
                        bias=attb[:, typ * 4 + r:typ * 4 + r + 1], scale=0.125), reads=[pS.b, attb.b], writes=[E_.b])
                    if typ == 1:
                        P.add("dve", lambda e, E_=E_, r=r: e.tensor_tensor(
                            out=E_[:], in0=E_[:], in1=maskeq[:, r, :].unsqueeze(1).broadcast_to([128, 2, 128]), op=ALU.mult),
                            reads=[E_.b, maskeq.b], writes=[E_.b])
                    for m in range(2):
                        P.add("pe", lambda e, m=m, kb=kb, E_=E_, accs=accs: e.matmul(
                            accs[m][:, 0:129], lhsT=E_[:, m, :], rhs=Va[:, kb, 0:129],
                            start=(kb == 0), stop=(kb == NKB - 1)), reads=[E_.b, Va.b], writes=[accs[m].b])
                k = i % 2
                P.add("dve", lambda e, accs=accs: e.reciprocal(out=rc[:, 0:1], in_=accs[0][:, 128:129]), reads=[accs[0].b], writes=[rc.b])
                P.add("dve", lambda e, accs=accs: e.reciprocal(out=rc[:, 1:2], in_=accs[1][:, 128:129]), reads=[accs[1].b, rc.b], writes=[rc.b])
                P.add("dve", lambda e: e.tensor_tensor(out=rc[:, 1:2], in0=rc[:, 1:2], in1=nlam[:], op=ALU.mult),
                      reads=[rc.b, nlam.b], writes=[rc.b])
                P.add("dve", lambda e, accs=accs: e.tensor_scalar(out=o1[:], in0=accs[0][:, 0:128], scalar1=rc[:, 0:1], scalar2=None,
                                                                  op0=ALU.mult), reads=[accs[0].b, rc.b], writes=[o1.b])
                P.add("dve", lambda e, accs=accs, k=k: e.scalar_tensor_tensor(
                    out=o2[k][:], in0=accs[1][:, 0:128], scalar=rc[:, 1:2], in1=o1[:], op0=ALU.mult, op1=ALU.add),
                    reads=[accs[1].b, rc.b, o1.b], writes=[o2[k].b])
                subln_store(128, o2[k][:, :], o2[k].b, oall.t[i * 128:(i + 1) * 128, h * 128:(h + 1) * 128], oall.b, osb[k], "pool")
        P.barrier()

    if KSTOP == 7:
        P.emit(); es.close(); return nc
    with contextlib.ExitStack() as s1:
        TQ = 32
        NQ = 128 // TQ
        selp = sb("selp", [NS, NGRP, 128], F32, s1)
        dma("sp", selp[:], selp_d.t[:, :].rearrange("(g s) q -> s g q", s=NS), [], [selp.b])
        selT = sb("selT", [128, NGRP, NS], F32, s1)
        dma("sp", selT[:], selT_d.t[:, :].rearrange("(g q) s -> q g s", q=128), [], [selT.b])
        blk = sb("blk", [128, 128], F32, s1)
        dma("sp", blk[:], blk_d[:, :], [], [blk.b])
        idx = sb("idx", [128, NGRP], I32, s1)
        dma("sp", idx[:], ptab.t[:, :], [], [idx.b])
        kg = [sb("kg", [128, TQ, 128], F32, s1) for _ in range(2)]
        pr = sb("pr", [128, TQ, 128], F32, s1)
        qb = sb("qb", [128, 128], F32, s1)
        sc = sb("sc", [128, 128, 2], F32, s1)
        Ee = sb("Ee", [128, 2, 128], F32, s1)
        den = sb("den", [128, 2], F32, s1); dn2 = sb("dn2", [128, 2], F32, s1)
        aw = sb("aw", [128, 128], F32, s1)
        num = sb("num", [128, 128], F32, s1); nq = sb("nq", [128, 128], F32, s1)
        snp = sb("snp", [NS, 128], F32, s1); sn = sb("sn", [NS, 2], F32, s1)
        P.add("dve", lambda e: e.tensor_tensor(out=snp[:], in0=qkvh[:, 0:128], in1=qkvh[:, 128:256], op=ALU.mult),
              reads=[qkvh.b], writes=[snp.b])
        P.add("dve", lambda e: e.tensor_reduce(out=sn[:], in_=snp[:].rearrange("s (m d) -> s m d", m=2), axis=AX.X, op=ALU.add),
              reads=[snp.b], writes=[sn.b])
        P.add("act", lambda e: e.activation(out=sn[:], in_=sn[:], func=AF.Exp, scale=0.125), reads=[sn.b], writes=[sn.b])
        anew = sb("anew", [NS, 2], F32, s1)
        dnall = sb("dnall", [NS, 2], F32, s1)
        gi = 0
        for g in range(NGRP):
            P.add("pe", lambda e, g=g: e.matmul(psb[0][:, 0:128], lhsT=selp[:, g, :], rhs=qkvh[:, 0:128], start=True, stop=True),
                  reads=[selp.b, qkvh.b], writes=[psb[0].b])
            copy("act", qb[:], psb[0][:, 0:128], [psb[0].b], [qb.b])
            P.add("pe", lambda e, g=g: e.matmul(psb[1][:, 0:2], lhsT=selp[:, g, :], rhs=sn[:, :], start=True, stop=True),
                  reads=[selp.b, sn.b], writes=[psb[1].b])
            for qq in range(NQ):
                k_ = kg[gi % 2]; gi += 1
                P.add("pool", lambda e, g=g, qq=qq, k_=k_: e.indirect_dma_start(
                    out=k_[:].rearrange("q t d -> q (t d)"), out_offset=None, in_=ckc.t[:, :],
                    in_offset=bass.IndirectOffsetOnAxis(ap=idx[:, g:g + 1], axis=0), element_offset=qq * TQ * 128),
                    reads=[idx.b], writes=[k_.b], dma=True)
                P.add("dve", lambda e, k_=k_: e.tensor_tensor(
                    out=pr[:], in0=k_[:], in1=qb[:, :].unsqueeze(1).broadcast_to([128, TQ, 128]), op=ALU.mult),
                    reads=[k_.b, qb.b], writes=[pr.b])
                P.add("dve", lambda e, qq=qq: e.tensor_reduce(
                    out=sc[:, qq * TQ:(qq + 1) * TQ, :], in_=pr[:].rearrange("q t (m d) -> q t m d", m=2), axis=AX.X, op=ALU.add),
                    reads=[pr.b], writes=[sc.b])
            for m in range(2):
                P.add("act", lambda e, m=m: e.activation(out=Ee[:, m, :], in_=sc[:, :, m], func=AF.Exp, scale=0.125,
                                                         accum_out=den[:, m:m + 1]), reads=[sc.b], writes=[Ee.b, den.b])
            P.add("pe", lambda e: e.matmul(psb[2][:, 0:2], lhsT=blk[:, :], rhs=den[:, :], start=True, stop=True),
                  reads=[blk.b, den.b], writes=[psb[2].b])
            P.add("act", lambda e: e.copy(out=dn2[:], in_=psb[1][:, 0:2]), reads=[psb[1].b], writes=[dn2.b])
            P.add("dve", lambda e: e.tensor_tensor(out=dn2[:], in0=psb[2][:, 0:2], in1=dn2[:], op=ALU.add),
                  reads=[psb[2].b, dn2.b], writes=[dn2.b])
            P.add("dve", lambda e: e.reciprocal(out=dn2[:], in_=dn2[:]), reads=[dn2.b], writes=[dn2.b])
            P.add("pe", lambda e, g=g: e.matmul(psb[3][0:NS, 0:2], lhsT=selT[:, g, :], rhs=dn2[:, :], start=(g == 0), stop=(g == NGRP - 1)),
                  reads=[selT.b, dn2.b], writes=[psb[3].b])
            P.add("dve", lambda e: e.tensor_tensor(out=dn2[:, 1:2], in0=dn2[:, 1:2], in1=nlam[:], op=ALU.mult),
                  reads=[dn2.b, nlam.b], writes=[dn2.b])
            P.add("dve", lambda e: e.tensor_scalar(out=aw[:], in0=Ee[:, 0, :], scalar1=dn2[:, 0:1], scalar2=None, op0=ALU.mult),
                  reads=[Ee.b, dn2.b], writes=[aw.b])
            P.add("dve", lambda e: e.scalar_tensor_tensor(out=aw[:], in0=Ee[:, 1, :], scalar=dn2[:, 1:2], in1=aw[:],
                                                          op0=ALU.mult, op1=ALU.add), reads=[Ee.b, dn2.b, aw.b], writes=[aw.b])
            for qq in range(NQ):
                v_ = kg[gi % 2]; gi += 1
                P.add("pool", lambda e, g=g, qq=qq, v_=v_: e.indirect_dma_start(
                    out=v_[:].rearrange("q t d -> q (t d)"), out_offset=None, in_=cvc.t[:, :],
                    in_offset=bass.IndirectOffsetOnAxis(ap=idx[:, g:g + 1], axis=0), element_offset=qq * TQ * 128),
                    reads=[idx.b], writes=[v_.b], dma=True)
                P.add("dve", lambda e, v_=v_, qq=qq: e.tensor_tensor(
                    out=pr[:], in0=v_[:], in1=aw[:, qq * TQ:(qq + 1) * TQ].unsqueeze(2).broadcast_to([128, TQ, 128]), op=ALU.mult),
                    reads=[v_.b, aw.b], writes=[pr.b])
                dstn = num if qq == 0 else nq
                P.add("dve", lambda e, dstn=dstn: e.tensor_reduce(
                    out=dstn[:], in_=pr[:].rearrange("q t d -> q d t"), axis=AX.X, op=ALU.add), reads=[pr.b], writes=[dstn.b])
                if qq > 0:
                    P.add("dve", lambda e: e.tensor_tensor(out=num[:], in0=num[:], in1=nq[:], op=ALU.add),
                          reads=[num.b, nq.b], writes=[num.b])
            P.add("pe", lambda e, g=g: e.matmul(psb[4][0:NS, 0:128], lhsT=selT[:, g, :], rhs=num[:, :], start=(g == 0), stop=(g == NGRP - 1)),
                  reads=[selT.b, num.b], writes=[psb[4].b])
        P.add("dve", lambda e: e.tensor_scalar(out=dnall[:], in0=psb[3][0:NS, 0:2], scalar1=1.0 / NPG, scalar2=None, op0=ALU.mult),
              reads=[psb[3].b], writes=[dnall.b])
        P.add("dve", lambda e: e.tensor_tensor(out=anew[:], in0=sn[:], in1=dnall[:], op=ALU.mult), reads=[sn.b, dnall.b], writes=[anew.b])
        P.add("dve", lambda e: e.scalar_tensor_tensor(out=anew[:, 0:1], in0=anew[:, 1:2], scalar=nlam[0:NS, 0:1], in1=anew[:, 0:1],
                                                      op0=ALU.mult, op1=ALU.add), reads=[anew.b, nlam.b], writes=[anew.b])
        osf = sb("osf", [NS, 128], F32, s1)
        P.add("dve", lambda e: e.scalar_tensor_tensor(out=osf[:], in0=qkvh[:, 256:384], scalar=anew[:, 0:1], in1=psb[4][0:NS, 0:128],
                                                      op0=ALU.mult, op1=ALU.add), reads=[qkvh.b, anew.b, psb[4].b], writes=[osf.b])
        osn = sb("osn", [NS, 128], F32, s1)
        subln_store(NS, osf[:, :], osf.b, os_in.t[:, :], os_in.b, osn, "pool")
        gather8(os_in, os_mid, os_out)
        P.barrier()

    if KSTOP == 8:
        P.emit(); es.close(); return nc
    with contextlib.ExitStack() as s1:
        wo = [[wunit(w_dout, kq * 512, n * 512) for kq in range(2)] for n in range(2)]
        gt_ = [sb("ot", [128, D], BF16, s1) for _ in range(2)]
        osl = sb("osl", [NS, D], F32, s1)
        gT = [sb("oT", [128, 8, 128], BF16, s1) for _ in range(2)]
        xr = [sb("xr1", [128, D], F32, s1) for _ in range(2)]
        for i, (r0, p) in enumerate(tiles(NS)):
            k = i % 2
            if r0 < NTC:
                dma("sp", gt_[k][0:p, :], oall.t[r0:r0 + p, :], [oall.b], [gt_[k].b])
                dma("sp", xr[k][0:p, :], y1.t[r0:r0 + p, :], [y1.b], [xr[k].b])
            else:
                dma("sp", osl[:, :].rearrange("s (h e) -> s h e", h=8), os_out.t[:, :].rearrange("(h s) e -> s h e", s=NS),
                    [os_out.b], [osl.b])
                P.add("dve", lambda e, k=k, p=p: e.tensor_copy(out=gt_[k][0:p, :], in_=osl[:, :]), reads=[osl.b], writes=[gt_[k].b])
                dma("sp", xr[k][0:p, :], ys_out.t[0:p, :], [ys_out.b], [xr[k].b])
            transposeT(gt_[k][0:p, :], gt_[k].b, p, 8, gT[k], gT[k].b, 0, 6 + k)
            for n in range(2):
                pb = psb[(2 * i + n) % 4]
                for kt in range(8):
                    P.add("pe", lambda e, kt=kt, n=n, k=k, p=p, pb=pb: e.matmul(
                        pb[0:p, :], lhsT=gT[k][:, kt, 0:p], rhs=wo[n][kt // 4][:, kt % 4, :],
                        start=(kt == 0), stop=(kt == 7)), reads=[gT[k].b] + [w.b for w in wo[n]], writes=[pb.b])
                P.add("dve", lambda e, n=n, k=k, p=p, pb=pb: e.tensor_tensor(
                    out=xr[k][0:p, n * 512:(n + 1) * 512], in0=xr[k][0:p, n * 512:(n + 1) * 512], in1=pb[0:p, :], op=ALU.add),
                    reads=[xr[k].b, pb.b], writes=[xr[k].b])
            dma("pool", ymid.t[r0:r0 + p, :], xr[k][0:p, :], [xr[k].b], [ymid.b])
        P.barrier()

    if KSTOP == 9:
        P.emit(); es.close(); return nc
    ffn_ple(1, NS, p1p, p1s, final=True)
    P.emit()
    es.close()
    return nc


def _tables(pos, idx_in_chunk):
    dk = 256
    angle = 1.0 / (10000.0 ** np.linspace(0.0, 1.0, dk // 2))
    angle = np.repeat(angle, 2)
    th = pos[:, None].astype(np.float64) * angle[None, :]
    th32 = (pos[:, None].astype(np.float32) * angle.astype(np.float32)[None, :]).astype(np.float64)
    cos = np.cos(th32); sin = np.sin(th32)
    sgn = np.where(np.arange(dk) % 2 == 0, -1.0, 1.0)
    sinS = sin * sgn[None, :]
    out = [np.zeros((len(pos), 1024), np.float32) for _ in range(4)]
    for h in range(4):
        lg = np.log1p(-2.0 ** (-5.0 - h))
        dq = np.exp(lg * (idx_in_chunk + 1.0))[:, None]
        dkk = np.exp(-lg * (idx_in_chunk + 1.0))[:, None] / 16.0
        sl = slice(h * 256, (h + 1) * 256)
        out[0][:, sl] = cos * dq; out[1][:, sl] = sinS * dq
        out[2][:, sl] = cos * dkk; out[3][:, sl] = sinS * dkk
    return out


_NC_CACHE = {}


def kernel(x_prompt, x_sample, state_ret, cache_k, cache_v, page_table, p_prompt, p_sample,
           norm_mix, ret_w_in, ret_w_out, diff_w_in, diff_w_out,
           diff_lambda_q1, diff_lambda_k1, diff_lambda_q2, diff_lambda_k2, diff_subln,
           norm_ffn, w_up, w_down, ple_norm, w_ple_gate, w_ple_proj, final_norm):
    f32 = np.float32
    A = lambda a: np.ascontiguousarray(np.asarray(a))
    x_prompt = A(x_prompt); cache_k = A(cache_k); cache_v = A(cache_v)
    B, SEQ, _ = x_prompt.shape
    NPHYS = cache_k.shape[1]
    NPG = page_table.shape[1]
    PAST = NPG * 128
    NTC = SEQ // 4
    GP = 128 // NPG
    NGRP = NS // GP
    key = (SEQ, PAST, NPHYS)
    if key not in _NC_CACHE:
        _NC_CACHE[key] = build(SEQ, PAST, NPHYS)
    nc = _NC_CACHE[key]
    bf = ml_dtypes.bfloat16
    ident = np.eye(128, dtype=f32).astype(bf)
    jj = np.arange(128)
    maskT = (jj[:, None] <= jj[None, :]).astype(f32).astype(bf)
    strict = (jj[:, None] < jj[None, :]).astype(f32)
    tabs = _tables(np.array([PAST], np.int64), np.array([0.0]))
    tabs = np.concatenate(tabs, 0).astype(f32)
    selp = np.zeros((NGRP, NS, 128), f32); selT = np.zeros((NGRP, 128, NS), f32)
    for g in range(NGRP):
        for q in range(128):
            s = g * GP + q // NPG
            selp[g, s, q] = 1.0; selT[g, q, s] = 1.0
    blk = (jj[:, None] // NPG == jj[None, :] // NPG).astype(f32)
    pt_all = np.ascontiguousarray(A(page_table).astype(np.int32).reshape(NGRP, 128).T)
    lam4 = np.stack([A(diff_lambda_q1)[0], A(diff_lambda_k1)[0], A(diff_lambda_q2)[0], A(diff_lambda_k2)[0]]).astype(f32)
    in_maps = []
    for c in range(8):
        b, cp = c // 4, c % 4
        sl = slice(cp * NTC, (cp + 1) * NTC)
        pos = np.arange(cp * NTC, (cp + 1) * NTC)
        tb = _tables(pos, (pos % 128).astype(np.float64))
        coef = np.zeros((1, 16), f32)
        for r in range(4):
            for h in range(4):
                if r < cp:
                    coef[0, r * 4 + h] = GAM[h] ** (NTC * (cp - 1 - r))
        attb = np.zeros((1, 12), f32)
        meq = np.zeros((128, 4, 128), f32)
        for r in range(4):
            attb[0, 0 * 4 + r] = 0.0 if r <= cp else NEG
            attb[0, 1 * 4 + r] = 0.0 if r <= cp else NEG
            attb[0, 2 * 4 + r] = 0.0 if r < cp else NEG
            meq[:, r, :] = 1.0 if r < cp else (np.asarray(maskT, f32) if r == cp else 0.0)
        ss = slice(c * NSL, (c + 1) * NSL)
        hs = slice(c * 128, (c + 1) * 128)
        dwi = A(diff_w_in)[0]
        m = {
            "xp": A(x_prompt[b, sl]), "xs": A(x_sample)[ss, 0],
            "p0p": A(p_prompt)[0, b, sl], "p1p": A(p_prompt)[1, b, sl],
            "p0s": A(p_sample)[0, ss, 0], "p1s": A(p_sample)[1, :, 0],
            "st": A(state_ret)[0, ss].reshape(NSL, 1024, 512),
            "ckc": np.ascontiguousarray(cache_k[0, :, :, c, :]).reshape(NPHYS, 16384),
            "cvc": np.ascontiguousarray(cache_v[0, :, :, c, :]).reshape(NPHYS, 16384),
            "ptab": pt_all,
            "w_rin": A(ret_w_in)[0], "w_rout": A(ret_w_out)[0], "w_din": dwi,
            "w_dh": np.ascontiguousarray(np.concatenate([dwi[:, hs], dwi[:, 1024 + c * 128:1024 + (c + 1) * 128],
                                                         dwi[:, 2048 + c * 128:2048 + (c + 1) * 128]], 1)),
            "w_dout": A(diff_w_out)[0],
            "w_up0": A(w_up)[0], "w_up1": A(w_up)[1], "w_dn0": A(w_down)[0], "w_dn1": A(w_down)[1],
            "w_gt0": A(w_ple_gate)[0], "w_gt1": A(w_ple_gate)[1], "w_pj0": A(w_ple_proj)[0], "w_pj1": A(w_ple_proj)[1],
            "g_mix": A(norm_mix), "g_ffn": A(norm_ffn), "g_ple": A(ple_norm),
            "g_fin": A(final_norm).reshape(1, D), "g_sub": A(diff_subln).reshape(1, 128),
            "lam4": lam4, "ident": ident,
            "tab0": tb[0], "tab1": tb[1], "tab2": tb[2], "tab3": tb[3], "tabs": tabs,
            "maskT": maskT, "coef": coef, "attb": attb, "maskeq": meq.reshape(128, 512).astype(bf),
            "selp": selp.reshape(NGRP * NS, 128), "selT": selT.reshape(NGRP * 128, NS),
            "oh4": np.eye(NSL, dtype=f32), "blk": blk,
        }
        in_maps.append({k: np.ascontiguousarray(v) for k, v in m.items()})
    res = run_bass_kernel_spmd(nc, in_maps, core_ids=list(range(8)))
    R = res.results
    y_prompt = np.stack([np.concatenate([R[4 * b + cp]["y_p"] for cp in range(4)], 0) for b in range(B)]).astype(f32)
    y_sample = R[0]["y_s"].reshape(NS, 1, D).astype(f32)
    rsp = np.stack([R[4 * b + 3]["retp"].reshape(4, 256, 512) for b in range(B)])[None].astype(f32)
    rss = np.concatenate([R[c]["rets"].reshape(NSL, 4, 256, 512) for c in range(8)], 0)[None].astype(f32)
    kp = np.stack([np.concatenate([R[4 * b + cp]["kp_o"] for cp in range(4)], 0) for b in range(B)]).reshape(1, B, SEQ, 8, 128)
    vp = np.stack([np.concatenate([R[4 * b + cp]["vp_o"] for cp in range(4)], 0) for b in range(B)]).reshape(1, B, SEQ, 8, 128)
    ks = R[0]["ks_o"].reshape(1, NS, 1, 8, 128)
    vs = R[0]["vs_o"].reshape(1, NS, 1, 8, 128)
    return (y_prompt, y_sample, rsp, rss, kp.astype(f32), vp.astype(f32), ks.astype(f32), vs.astype(f32))
```
